# Optimizing a Trainium2 kernel written in Bass

```python
import jax, jax.numpy as jnp
from jax import lax
import numpy as np

D_MODEL = 2048
BATCH = 4
SEQ = 8192
DEPTH = 2

HEAD_DIM = 128
N_HEADS = D_MODEL // HEAD_DIM
N_HEADS_DIL = N_HEADS // 2
N_HEADS_NA = N_HEADS - N_HEADS_DIL
ROT_DIM = HEAD_DIM // 4
ROPE_THETA = 500000.0
DIL_PATTERNS = ((128, 1), (512, 4), (2048, 16))
BAND_BLOCK = 64
GRID_W = 64
NA_ROWS_MAX = 8
NA_COLS = 16
NA_Q_COLS = 16
NA_K_COLS = 32
D_FF = 4 * D_MODEL
EPS = 1e-6
NEG_INF = -1e30
W_DIL = 3 * N_HEADS_DIL * HEAD_DIM
W_NA = 3 * N_HEADS_NA * HEAD_DIM

kernel_name = 'hybrid_dilated_neighborhood_encoder'


def rms_norm_f32(x, g):
    xf = x.astype(jnp.float32)
    y = xf * lax.rsqrt(jnp.mean(xf * xf, axis=-1, keepdims=True) + EPS)
    return y * g.astype(jnp.float32)


def partial_rope(x):
    S = x.shape[1]
    pos = jnp.arange(S, dtype=jnp.float32)
    inv_freq = ROPE_THETA ** (-jnp.arange(0, ROT_DIM, 2, dtype=jnp.float32) / ROT_DIM)
    ang = pos[:, None] * inv_freq[None, :]
    cos = jnp.cos(ang)[None, :, None, :]
    sin = jnp.sin(ang)[None, :, None, :]
    half = ROT_DIM // 2
    x1 = x[..., :half]
    x2 = x[..., half:ROT_DIM]
    return jnp.concatenate([x1 * cos - x2 * sin, x2 * cos + x1 * sin, x[..., ROT_DIM:]], axis=-1)


def dilated_window_attn(q, k, v, window, dilation):
    B, S, H, hd = q.shape
    half = window // (2 * dilation)
    L = S // dilation
    nb = -(-L // BAND_BLOCK)
    Lp = nb * BAND_BLOCK
    pad = Lp - L

    def to_res(t):
        return t.reshape(B, L, dilation, H, hd).transpose(0, 2, 3, 1, 4)

    qr = jnp.pad(to_res(q), ((0, 0), (0, 0), (0, 0), (0, pad), (0, 0)))
    qb = qr.reshape(B, dilation, H, nb, BAND_BLOCK, hd)

    def band(t):
        tp = jnp.pad(to_res(t), ((0, 0), (0, 0), (0, 0), (BAND_BLOCK, BAND_BLOCK + pad), (0, 0)))
        tp = tp.reshape(B, dilation, H, nb + 2, BAND_BLOCK, hd)
        return jnp.concatenate([tp[:, :, :, :-2], tp[:, :, :, 1:-1], tp[:, :, :, 2:]], axis=4)

    kn = band(k)
    vn = band(v)
    n_i = np.arange(nb)[:, None, None]
    q_i = np.arange(BAND_BLOCK)[None, :, None]
    k_i = np.arange(3 * BAND_BLOCK)[None, None, :]
    rel = k_i - BAND_BLOCK - q_i
    key_abs = (n_i - 1) * BAND_BLOCK + k_i
    valid = (np.abs(rel) <= half) & (key_abs >= 0) & (key_abs < L)

    s = jnp.einsum('bzhnqd,bzhnkd->bzhnqk', qb, kn)
    s = jnp.where(valid, s, NEG_INF)
    m = jnp.max(s, axis=-1)
    p = jnp.exp(s - m[..., None])
    l = jnp.sum(p, axis=-1)
    o = jnp.einsum('bzhnqk,bzhnkd->bzhnqd', p, vn) / l[..., None]

    o = o.reshape(B, dilation, H, Lp, hd)[:, :, :, :L].transpose(0, 3, 1, 2, 4).reshape(B, S, H, hd)
    m = m.reshape(B, dilation, H, Lp)[..., :L].transpose(0, 3, 1, 2).reshape(B, S, H)
    l = l.reshape(B, dilation, H, Lp)[..., :L].transpose(0, 3, 1, 2).reshape(B, S, H)
    return o, m, l


def dilated_mixer(q, k, v):
    outs = [dilated_window_attn(q, k, v, w, d) for (w, d) in DIL_PATTERNS]
    m_all = jnp.stack([o[1] for o in outs], axis=0)
    m_max = jnp.max(m_all, axis=0)
    wts = jnp.stack([o[2] for o in outs], axis=0) * jnp.exp(m_all - m_max[None])
    o_all = jnp.stack([o[0] for o in outs], axis=0)
    return jnp.sum(wts[..., None] * o_all, axis=0) / jnp.sum(wts, axis=0)[..., None]


def neighborhood_attn(q, k, v, rpb):
    B, S, H, hd = q.shape
    rows = S // GRID_W
    kr = min(NA_ROWS_MAX, rows)
    ncb = GRID_W // NA_Q_COLS

    kstart = np.clip(np.arange(ncb) * NA_Q_COLS - NA_COLS // 2, 0, GRID_W - NA_K_COLS)
    colidx = kstart[:, None] + np.arange(NA_K_COLS)[None, :]
    qcol = (np.arange(ncb)[:, None] * NA_Q_COLS + np.arange(NA_Q_COLS)[None, :])[:, :, None]
    keycol = colidx[:, None, :]
    cs = np.clip(qcol - NA_COLS // 2, 0, GRID_W - NA_COLS)
    colmask = (keycol >= cs) & (keycol < cs + NA_COLS)
    coff = np.clip(keycol - qcol + NA_COLS - 1, 0, 2 * NA_COLS - 2)
    rpb_c = rpb.astype(jnp.float32)[:, :, coff]

    kg = k.reshape(B, rows, GRID_W, H, hd).transpose(0, 3, 1, 2, 4)
    vg = v.reshape(B, rows, GRID_W, H, hd).transpose(0, 3, 1, 2, 4)
    qg = q.reshape(B, rows, GRID_W, H, hd).transpose(1, 0, 3, 2, 4)

    def row_step(args):
        i, q_row = args
        rs = jnp.clip(i - kr // 2, 0, rows - kr)
        k_blk = lax.dynamic_slice_in_dim(kg, rs, kr, axis=2)[:, :, :, colidx]
        v_blk = lax.dynamic_slice_in_dim(vg, rs, kr, axis=2)[:, :, :, colidx]
        qb = q_row.reshape(B, H, ncb, NA_Q_COLS, hd)
        s = jnp.einsum('bhcqd,bhrckd->bhcqrk', qb, k_blk)
        roff = rs + jnp.arange(kr) - i + NA_ROWS_MAX - 1
        bias = rpb_c[:, roff].transpose(0, 2, 3, 1, 4)
        s = jnp.where(colmask[:, :, None, :], s + bias[None], NEG_INF)
        p = jax.nn.softmax(s.reshape(B, H, ncb, NA_Q_COLS, kr * NA_K_COLS), axis=-1)
        p = p.reshape(B, H, ncb, NA_Q_COLS, kr, NA_K_COLS)
        o = jnp.einsum('bhcqrk,bhrckd->bhcqd', p, v_blk)
        return o.reshape(B, H, GRID_W, hd)

    out = lax.map(row_step, (jnp.arange(rows), qg))
    return out.transpose(1, 0, 3, 2, 4).reshape(B, S, H, hd)


def setup_inputs(seed: int = 0) -> dict:
    key = jax.random.key(seed)
    ks = jax.random.split(key, 16)
    f32 = jnp.float32
    d = D_MODEL
    nrel_r = 2 * NA_ROWS_MAX - 1
    nrel_c = 2 * NA_COLS - 1
    return {
        'x': jax.random.normal(ks[0], (BATCH, SEQ, d), f32),
        'c': jax.random.normal(ks[1], (BATCH, d), f32),
        'ln1': 1.0 + 0.05 * jax.random.normal(ks[2], (DEPTH, d), f32),
        'w_ada': 0.5 * d ** -0.5 * jax.random.normal(ks[3], (DEPTH, d, 6 * d), f32),
        'b_ada': 0.01 * jax.random.normal(ks[4], (DEPTH, 6 * d), f32),
        'w_in': d ** -0.5 * jax.random.normal(ks[5], (DEPTH, d, W_DIL + W_NA), f32),
        'q_norm_dil': 1.0 + 0.05 * jax.random.normal(ks[6], (DEPTH, HEAD_DIM), f32),
        'k_norm_dil': 1.0 + 0.05 * jax.random.normal(ks[7], (DEPTH, HEAD_DIM), f32),
        'q_norm_na': 1.0 + 0.05 * jax.random.normal(ks[8], (DEPTH, HEAD_DIM), f32),
        'k_norm_na': 1.0 + 0.05 * jax.random.normal(ks[9], (DEPTH, HEAD_DIM), f32),
        'na_rel_bias': 0.1 * jax.random.normal(ks[10], (DEPTH, N_HEADS_NA, nrel_r, nrel_c), f32),
        'w_out': d ** -0.5 * jax.random.normal(ks[11], (DEPTH, d, d), f32),
        'ln2': 1.0 + 0.05 * jax.random.normal(ks[12], (DEPTH, d), f32),
        'w_mlp_in': d ** -0.5 * jax.random.normal(ks[13], (DEPTH, d, D_FF), f32),
        'w_mlp_out': D_FF ** -0.5 * jax.random.normal(ks[14], (DEPTH, D_FF, d), f32),
    }


def reference(x, c, ln1, w_ada, b_ada, w_in, q_norm_dil, k_norm_dil, q_norm_na, k_norm_na,
              na_rel_bias, w_out, ln2, w_mlp_in, w_mlp_out):
    B, S, D = x.shape
    scale = HEAD_DIM ** -0.5
    c_act = jax.nn.silu(c)
    for layer in range(DEPTH):
        mod = (c_act @ w_ada[layer] + b_ada[layer]).astype(jnp.float32)
        sh1, sc1, g1, sh2, sc2, g2 = jnp.split(mod[:, None, :], 6, axis=-1)

        h = (rms_norm_f32(x, ln1[layer]) * (1.0 + sc1) + sh1).astype(x.dtype)
        proj = h @ w_in[layer]
        p_dil = proj[..., :W_DIL].astype(jnp.float32).reshape(B, S, 3, N_HEADS_DIL, HEAD_DIM)
        p_na = proj[..., W_DIL:].astype(jnp.float32).reshape(B, S, 3, N_HEADS_NA, HEAD_DIM)

        qa = partial_rope(rms_norm_f32(p_dil[:, :, 0], q_norm_dil[layer])) * scale
        ka = partial_rope(rms_norm_f32(p_dil[:, :, 1], k_norm_dil[layer]))
        out_a = dilated_mixer(qa, ka, p_dil[:, :, 2])

        qb = rms_norm_f32(p_na[:, :, 0], q_norm_na[layer]) * scale
        kb = rms_norm_f32(p_na[:, :, 1], k_norm_na[layer])
        out_b = neighborhood_attn(qb, kb, p_na[:, :, 2], na_rel_bias[layer])

        mixed = jnp.concatenate([out_a, out_b], axis=2).reshape(B, S, D).astype(x.dtype)
        x = (x + g1 * (mixed @ w_out[layer])).astype(x.dtype)

        h2 = (rms_norm_f32(x, ln2[layer]) * (1.0 + sc2) + sh2).astype(x.dtype)
        hid = jnp.square(jax.nn.relu(h2 @ w_mlp_in[layer]))
        x = (x + g2 * (hid @ w_mlp_out[layer])).astype(x.dtype)
    return x
```

```python
import numpy as np
import ml_dtypes
import concourse.bass as bass
import concourse.mybir as mybir
from concourse.bass_utils import run_bass_kernel_spmd

F32 = mybir.dt.float32
BF16 = mybir.dt.bfloat16
ALU = mybir.AluOpType
AF = mybir.ActivationFunctionType

D = 2048
NCH = 16
DFF = 8192
NFC = 64
HD = 128
NH = 16
EPS = 1e-6
SCALE = HD ** -0.5
TG = 512
NEG = -30000.0
ENGS = ("pe", "act", "dve", "pool", "sp")


class Op:
    __slots__ = ("eng", "emit", "chan", "deps", "signal", "ticket")

    def __init__(self, eng, emit, chan):
        self.eng = eng
        self.emit = emit
        self.chan = chan
        self.deps = []
        self.signal = False
        self.ticket = 0


class Sched:
    def __init__(self, nc):
        self.nc = nc
        self.ops = {e: [] for e in ENGS}
        self.last_writer = {}
        self.readers = {}
        self.chan_count = {}
        self.sync_same = {"act", "dve", "pool"}
        self.pending = {e: [] for e in ENGS}

    def add(self, eng, emit, reads=(), writes=(), chan=None):
        op = Op(eng, emit, chan)
        deps = {}
        lw = self.last_writer
        rd = self.readers
        for b in reads:
            w = lw.get(b)
            if w is not None:
                deps[id(w)] = w
        for b in writes:
            w = lw.get(b)
            if w is not None:
                if not (chan is not None and w.chan == chan and not rd.get(b)):
                    deps[id(w)] = w
            for r in rd.get(b, {}).values():
                deps[id(r)] = r
        dl = list(self.pending[eng])
        self.pending[eng] = []
        for w in deps.values():
            if w.chan is None:
                if w.eng == eng and eng not in self.sync_same:
                    continue
                w.signal = True
                dl.append(w)
            else:
                dl.append((("c", w.chan), self.chan_count[w.chan]))
        op.deps = dl
        if chan is not None:
            self.chan_count[chan] = self.chan_count.get(chan, 0) + 16
            op.ticket = self.chan_count[chan]
        rk = (eng, chan)
        for b in reads:
            rd.setdefault(b, {})[rk] = op
        for b in writes:
            lw[b] = op
            rd[b] = {}
        self.ops[eng].append(op)
        return op

    def fence(self):
        deps = []
        for e in ENGS:
            for op in reversed(self.ops[e]):
                if op.chan is None:
                    op.signal = True
                    deps.append(op)
                    break
        for c, v in self.chan_count.items():
            deps.append((("c", c), v))
        for e in ENGS:
            self.pending[e] = list(deps)
        self.last_writer = {}
        self.readers = {}

    def emit_all(self, block, sem_of):
        for e in ENGS:
            c = 0
            for op in self.ops[e]:
                if op.chan is None and op.signal:
                    c += 1
                    op.ticket = c

        def run(engname, eng):
            seen = {}
            for op in self.ops[engname]:
                for d in op.deps:
                    if isinstance(d, Op):
                        if d.eng == engname and engname == "pe":
                            continue
                        key, val = ("e", d.eng), d.ticket
                    else:
                        key, val = d
                    if seen.get(key, 0) >= val:
                        continue
                    seen[key] = val
                    eng.wait_ge(sem_of(key), val)
                ins = op.emit(eng)
                if op.chan is not None:
                    ins.then_inc(sem_of(("c", op.chan)), 16)
                elif op.signal:
                    ins.then_inc(sem_of(("e", engname)), 1)
            for c, v in self.chan_count.items():
                key = ("c", c)
                if seen.get(key, 0) < v:
                    seen[key] = v
                    eng.wait_ge(sem_of(key), v)

        @block.tensor
        def _(e):
            run("pe", e)

        @block.scalar
        def _(e):
            run("act", e)

        @block.vector
        def _(e):
            run("dve", e)

        @block.gpsimd
        def _(e):
            run("pool", e)

        @block.sync
        def _(e):
            run("sp", e)


def dil_mask_table():
    k = np.arange(128)[:, None]
    q = np.arange(128)[None, :]
    out = np.zeros((128, 17, 128), np.float32)
    for j in range(17):
        dlt = -1024 + 128 * j + k - q
        a = np.abs(dlt)
        c = (a <= 64).astype(np.float32) + ((dlt % 4 == 0) & (a <= 256)) + ((dlt % 16 == 0) & (a <= 1024))
        out[:, j, :] = c
    return out.reshape(128, 17 * 128).astype(ml_dtypes.bfloat16)


NA_TILES = ([(10, 10 + o) for o in (-2, -1, 0, 1, 2)] + [(0, c) for c in (0, 1, 2, 3)] + [(1, c) for c in (0, 1, 2, 3)])


def na_chunks(qt):
    if qt == 0:
        return [(c, 5 + c) for c in range(4)]
    if qt == 1:
        return [(c, 9 + c) for c in range(4)]
    return [(qt + o, 2 + o) for o in (-2, -1, 0, 1, 2)]


def na_index_tables(S, rev):
    rows = S // 64
    allowed = np.zeros((13, 128, 128), bool)
    roff = np.zeros((13, 128, 128), np.int64)
    coff = np.zeros((13, 128, 128), np.int64)
    loc = np.arange(128)
    for ti, (qt, kc) in enumerate(NA_TILES):
        ql = qt * 128 + loc
        kl = kc * 128 + loc
        qg = (S - 1 - ql) if rev else ql
        kg = (S - 1 - kl) if rev else kl
        qi, qc = qg // 64, qg % 64
        ki, kcc = kg // 64, kg % 64
        rs = np.clip(qi - 4, 0, rows - 8)
        cs = np.clip(qc - 8, 0, 64 - 16)
        al = ((ki[:, None] >= rs[None, :]) & (ki[:, None] < rs[None, :] + 8) &
              (kcc[:, None] >= cs[None, :]) & (kcc[:, None] < cs[None, :] + 16))
        ro = ki[:, None] - qi[None, :] + 7
        co = np.clip(kcc[:, None] - qc[None, :] + 15, 0, 30)
        allowed[ti] = al
        roff[ti] = np.clip(ro, 0, 14)
        coff[ti] = co
    return allowed, roff, coff


def rope_tables(S, rev, n):
    l = np.arange(n)
    pos = ((S - 1 - l) if rev else l).astype(np.float32)
    inv = (np.float32(500000.0) ** (-np.arange(0, 32, 2, dtype=np.float32) / np.float32(32))).astype(np.float32)
    ang = pos[None, :] * inv[:, None]
    cos = np.cos(ang).astype(np.float32)
    sin = np.sin(ang).astype(np.float32)
    return np.concatenate([cos, cos], 0), np.concatenate([sin, sin], 0)


def rot_matrix():
    R = np.zeros((32, 32), np.float32)
    for dp in range(16):
        R[dp + 16, dp] = -1.0
        R[dp, dp + 16] = 1.0
    return R


def relayout_w(w, ncol_chunk):
    K, N = w.shape
    return np.ascontiguousarray(w.reshape(K // 128, 128, N // ncol_chunk, ncol_chunk).transpose(2, 1, 0, 3))


def vecT(v):
    return np.ascontiguousarray(v.reshape(-1, 128).T)


def build_program(depth, NQF, debug=False):
    NQ = [NQF + 1024 * (depth - 1 - l) for l in range(depth)]
    NK = [q + 1024 for q in NQ]
    NX = NK[0]
    nc = bass.Bass("TRN2", target_bir_lowering=False)
    S = Sched(nc)
    sems = {}

    def sem_of(key):
        if key not in sems:
            sems[key] = nc.alloc_semaphore("s%d" % len(sems))
        return sems[key]

    def din(name, shape, dt=F32):
        return nc.dram_tensor(name, list(shape), dt, kind="ExternalInput").ap()

    def dscr(name, shape, dt):
        return nc.dram_tensor(name, list(shape), dt).ap()

    xT_in = din("xT", [D, NX])
    cT_in = din("cT", [128, NCH])
    wada_in = din("w_ada_r", [depth, 96, 128, D])
    badaT_in = din("b_adaT", [depth, 128, 96])
    lnT_in = din("lnT", [depth, 2, 128, NCH])
    gn_in = din("gn", [depth, 128, 4])
    wqk_in = din("wqk_r", [depth, 32, 128, D])
    wv_in = din("wv_r", [depth, 4, 128, NCH * 512])
    wo_in = din("wo_r", [depth, NCH, 128, D])
    wmi_in = din("wmi_r", [depth, NFC, 128, D])
    wmo_in = din("wmo_r", [depth, NCH, 128, NFC * 128])
    nab_in = din("nabias", [depth, 8, 128, 13 * 128])
    dmask_in = din("dmask", [128, 17 * 128], BF16)
    cos_in = din("cosT", [32, NX])
    sin_in = din("sinT", [32, NX])
    rot_in = din("rotm", [32, 32])
    outT = nc.dram_tensor("outT", [D, NQF], F32, kind="ExternalOutput").ap()

    wqk_bf = dscr("wqk_bf", [depth, 32, 128, D], BF16)
    wv_bf = dscr("wv_bf", [depth, 4, 128, NCH * 512], BF16)
    wo_bf = dscr("wo_bf", [depth, NCH, 128, D], BF16)
    wmi_bf = dscr("wmi_bf", [depth, NFC, 128, D], BF16)
    wmo_bf = dscr("wmo_bf", [depth, NCH, 128, NFC * 128], BF16)
    qT_scr = dscr("qT_scr", [NH, 128, NX], BF16)
    kT_scr = dscr("kT_scr", [NH, 128, NX], BF16)
    v_scr = dscr("v_scr", [NX, D], BF16)
    oT_scr = dscr("oT_scr", [D, NQ[0]], BF16)
    xa = dscr("xa", [D, NQ[0]], F32)

    def sb(name, shape, dt):
        return nc.alloc_sbuf_tensor(name, list(shape), dt)

    ones32 = sb("ones32", [128, 128], F32)
    onesbf = sb("onesbf", [128, 128], BF16)
    cact = sb("cact", [128, NCH], F32)
    modT = [sb("modT%d" % l, [128, 96], F32) for l in range(depth)]
    lnT = [sb("lnT%d" % l, [128, 2 * NCH], F32) for l in range(depth)]
    G = [sb("G%d" % l, [128, 2 * NCH], F32) for l in range(depth)]
    gn = [sb("gn%d" % l, [128, 4], F32) for l in range(depth)]
    rotm = sb("rotm_s", [32, 32], F32)
    dmask = sb("dmask_s", [128, 17 * 128], BF16)

    A32 = sb("A32", [128, NCH * TG], F32)
    B16 = sb("B16", [128, NCH * TG], BF16)
    BIG = sb("BIG", [128, 36864], BF16)
    WR = sb("WR", [128, 4 * 2048], BF16)
    WV = sb("WV", [128, 2 * 8192], BF16)
    SQ = sb("SQ", [128, 2 * TG], F32)
    RS = sb("RS", [128, 2 * TG], F32)
    TMP = sb("TMP", [128, 4 * TG], F32)
    OB = sb("OB", [128, 4 * TG], BF16)
    OST = sb("OST", [128, 2 * TG], BF16)
    CS = sb("CS", [32, 2 * TG], F32)
    NAB = A32[:, 0:2 * 13 * 128]
    RC = A32[:, 4096:4096 + 2 * 128]

    banks = [nc.alloc_psum_tensor("bank%d" % i, [128, TG], F32) for i in range(8)]

    xg = A32[:].rearrange("p (c n) -> p c n", c=NCH)
    hT = B16[:].rearrange("p (c n) -> p c n", c=NCH)
    mix = hT

    class Ring:
        def __init__(self, name, n):
            self.name, self.n, self.i = name, n, 0

        def next(self):
            k = self.i % self.n
            self.i += 1
            return k, (self.name, k)

    r_wr, r_wv, r_sq, r_rs, r_tmp, r_ob, r_rc = (Ring("WR", 4), Ring("WV", 2), Ring("SQ", 2), Ring("RS", 2),
                                                 Ring("TMP", 4), Ring("OB", 4), Ring("RC", 2))
    r_bank = Ring("bank", 4)
    r_st2 = Ring("bank2", 2)
    r_sc = Ring("sc", 2)
    r_ost = Ring("OST", 2)
    r_ada = Ring("ADA", 2)

    def bank_main():
        k, _ = r_bank.next()
        return banks[k], ("bank", k)

    def dma(eng, out, in_, reads, writes, chan):
        S.add(eng, lambda e: e.dma_start(out=out, in_=in_), reads=reads, writes=writes, chan=chan)

    def mm(out, lhsT, rhs, start, stop, reads, writes):
        S.add("pe", lambda e: e.matmul(out, lhsT, rhs, start=start, stop=stop), reads=reads, writes=writes)

    def act(out, in_, func, reads, writes, bias=None, scale=None):
        kw = {}
        if bias is not None:
            kw["bias"] = bias
        if scale is not None:
            kw["scale"] = scale
        S.add("act", lambda e: e.activation(out=out, in_=in_, func=func, **kw), reads=reads, writes=writes)

    def tt(out, in0, in1, op, reads, writes, eng="dve"):
        S.add(eng, lambda e: e.tensor_tensor(out=out, in0=in0, in1=in1, op=op), reads=reads, writes=writes)

    def stt(out, in0, scalar, in1, op0, op1, reads, writes, eng="dve"):
        S.add(eng, lambda e: e.scalar_tensor_tensor(out=out, in0=in0, scalar=scalar, in1=in1, op0=op0, op1=op1),
              reads=reads, writes=writes)

    def ts(out, in0, s1, s2, op0, op1, reads, writes, eng="dve"):
        S.add(eng, lambda e: e.tensor_scalar(out=out, in0=in0, scalar1=s1, scalar2=s2, op0=op0, op1=op1),
              reads=reads, writes=writes)

    def recip(out, in_, reads, writes):
        S.add("dve", lambda e: e.reciprocal(out=out, in_=in_), reads=reads, writes=writes)

    def memset(t, v, key):
        S.add("dve", lambda e: e.memset(t, v), writes=[key])

    memset(ones32[:], 1.0, "ones32")
    memset(onesbf[:], 1.0, "onesbf")
    dma("sp", cact[:], cT_in, [], ["cact"], "const")
    dma("sp", rotm[:], rot_in, [], ["rotm"], "const")
    dma("sp", dmask[:], dmask_in, [], ["dmask"], "const")
    for l in range(depth):
        dma("sp", lnT[l][:, 0:NCH], lnT_in[l, 0], [], [("lnT", l)], "const")
        dma("sp", lnT[l][:, NCH:2 * NCH], lnT_in[l, 1], [], [("lnT", l)], "const")
        dma("sp", gn[l][:], gn_in[l], [], [("gn", l)], "const")
        dma("sp", modT[l][:], badaT_in[l], [], [("modT", l)], "const")

    def cast_copy(dst, src, rows_per=1024):
        d2 = dst.rearrange("a p n -> (a p) n")
        s2 = src.rearrange("a p n -> (a p) n")
        R, C = d2.shape
        if C > 2048:
            d2 = d2.rearrange("r (a n) -> (r a) n", n=2048)
            s2 = s2.rearrange("r (a n) -> (r a) n", n=2048)
            R = d2.shape[0]
        for r0 in range(0, R, rows_per):
            r1 = min(R, r0 + rows_per)
            dma("pool", d2[r0:r1, :], s2[r0:r1, :], [], ["wcast"], "wcast")

    for l in range(depth):
        cast_copy(wqk_bf[l], wqk_in[l])
        cast_copy(wv_bf[l], wv_in[l])
        cast_copy(wo_bf[l], wo_in[l])
        cast_copy(wmi_bf[l], wmi_in[l])
        cast_copy(wmo_bf[l], wmo_in[l])

    act(cact[:], cact[:], AF.Silu, ["cact"], ["cact"])
    adar = A32[:].rearrange("p (s j n) -> p s j n", s=2, j=2)
    mps = banks[4]
    for l in range(depth):
        for j0 in range(0, 96, 2):
            k, key = r_ada.next()
            dma("sp", adar[:, k], wada_in[l, j0:j0 + 2].rearrange("j p n -> p j n"), [], [key], key)
            for jj in range(2):
                j = j0 + jj
                for kc in range(NCH):
                    mm(mps[:, j:j + 1], adar[:, k, jj, kc * 128:(kc + 1) * 128], cact[:, kc:kc + 1],
                       kc == 0, kc == NCH - 1, [key, "cact"], [("bank", 4)])
        tt(modT[l][:], mps[:, 0:96], modT[l][:], ALU.add, [("bank", 4), ("modT", l)], [("modT", l)])
        stt(G[l][:, 0:NCH], modT[l][:, 16:32], 1.0, lnT[l][:, 0:NCH], ALU.add, ALU.mult,
            [("modT", l), ("lnT", l)], [("G", l)])
        stt(G[l][:, NCH:2 * NCH], modT[l][:, 64:80], 1.0, lnT[l][:, NCH:2 * NCH], ALU.add, ALU.mult,
            [("modT", l), ("lnT", l)], [("G", l)])
        ts(gn[l][:, 0:1], gn[l][:, 0:1], SCALE, 0.0, ALU.mult, ALU.add, [("gn", l)], [("gn", l)])
        ts(gn[l][:, 2:3], gn[l][:, 2:3], SCALE, 0.0, ALU.mult, ALU.add, [("gn", l)], [("gn", l)])
    S.fence()

    def norm_group(l, which):
        st = banks[4]
        for c in range(NCH):
            k, key = r_sq.next()
            sq = SQ[:, k * TG:(k + 1) * TG]
            act(sq, xg[:, c, :], AF.Square, [("xg", c)], [key])
            mm(st[:], ones32[:], sq, c == 0, c == NCH - 1, [key, "ones32"], [("bank", 4)])
        k, rkey = r_rs.next()
        rs = RS[:, k * TG:(k + 1) * TG]
        act(rs, st[:], AF.Sqrt, [("bank", 4)], [rkey], bias=EPS, scale=1.0 / D)
        recip(rs, rs, [rkey], [rkey])
        gcol = which * NCH
        shcol = 0 if which == 0 else 48
        for c in range(NCH):
            k2, tkey = r_tmp.next()
            tmp = TMP[:, k2 * TG:(k2 + 1) * TG]
            tt(tmp, xg[:, c, :], rs, ALU.mult, [("xg", c), rkey], [tkey])
            ts(hT[:, c, :], tmp, G[l][:, gcol + c:gcol + c + 1], modT[l][:, shcol + c:shcol + c + 1],
               ALU.mult, ALU.add, [tkey, ("G", l), ("modT", l)], [("hT", c)])

    def load_xg(src, t):
        for c0 in range(0, NCH, 4):
            dma("sp", xg[:, c0:c0 + 4, :],
                src[c0 * 128:(c0 + 4) * 128, t * TG:(t + 1) * TG].rearrange("(c p) n -> p c n", p=128),
                [], [("xg", c) for c in range(c0, c0 + 4)], "xg")

    def load_w(dram_chunk):
        k, key = r_wr.next()
        dma("sp", WR[:, k * 2048:(k + 1) * 2048], dram_chunk, [], [key], key)
        return WR[:, k * 2048:(k + 1) * 2048].rearrange("p (c n) -> p c n", c=NCH), key

    def qk_post(l, hh, qk, bk, bkey, tok):
        k, skey = r_sq.next()
        sq = SQ[:, k * TG:(k + 1) * TG]
        act(sq, bk[:], AF.Square, [bkey], [skey])
        k2, _ = r_st2.next()
        st2, st2key = banks[5 + k2], ("bank", 5 + k2)
        mm(st2[:], ones32[:], sq, True, True, [skey, "ones32"], [st2key])
        k3, rkey = r_rs.next()
        rs = RS[:, k3 * TG:(k3 + 1) * TG]
        act(rs, st2[:], AF.Sqrt, [st2key], [rkey], bias=EPS, scale=1.0 / HD)
        recip(rs, rs, [rkey], [rkey])
        gcol = (0 if hh < 8 else 2) + qk
        gsc = gn[l][:, gcol:gcol + 1]
        k4, okey = r_ob.next()
        ob = OB[:, k4 * TG:(k4 + 1) * TG]
        if hh >= 8:
            stt(ob, bk[:], gsc, rs, ALU.mult, ALU.mult, [bkey, rkey, ("gn", l)], [okey])
        else:
            k5, tkey = r_tmp.next()
            qn = TMP[:, k5 * TG:(k5 + 1) * TG]
            stt(qn, bk[:], gsc, rs, ALU.mult, ALU.mult, [bkey, rkey, ("gn", l)], [tkey])
            mm(banks[7][0:32, :], rotm[:], qn[0:32, :], True, True, [tkey, "rotm"], [("bank", 7)])
            k6, t2key = r_tmp.next()
            t1 = TMP[0:32, k6 * TG:(k6 + 1) * TG]
            tt(t1, banks[7][0:32, :], CS[:, TG:2 * TG], ALU.mult, [("bank", 7), "sin"], [t2key])
            tt(qn[0:32, :], qn[0:32, :], CS[:, 0:TG], ALU.mult, [tkey, "cos"], [tkey])
            tt(qn[0:32, :], qn[0:32, :], t1, ALU.add, [tkey, t2key], [tkey])
            S.add("act", lambda e: e.copy(ob, qn), reads=[tkey], writes=[okey])
        dst = (qT_scr if qk == 0 else kT_scr)[hh][:, tok]
        dma("pool", dst, ob, [okey], [("qk_scr", qk, hh)], "st_qk")

    def phase1(l, xsrc, nq, nk):
        for t in range(nk // TG):
            tok = slice(t * TG, (t + 1) * TG)
            load_xg(xsrc, t)
            dma("sp", CS[:, 0:TG], cos_in[:, tok], [], ["cos"], "cs")
            dma("sp", CS[:, TG:2 * TG], sin_in[:, tok], [], ["sin"], "cs")
            norm_group(l, 0)
            for hh in range(NH):
                for qk in range(2):
                    if qk == 0 and t * TG >= nq:
                        continue
                    w, wkey = load_w(wqk_bf[l, 2 * hh + qk])
                    bk, bkey = bank_main()
                    for kc in range(NCH):
                        mm(bk[:], w[:, kc, :], hT[:, kc, :], kc == 0, kc == NCH - 1, [wkey, ("hT", kc)], [bkey])
                    qk_post(l, hh, qk, bk, bkey, tok)
            for vb in range(4):
                k, vkey = r_wv.next()
                wv = WV[:, k * 8192:(k + 1) * 8192]
                dma("sp", wv, wv_bf[l, vb], [], [vkey], vkey)
                wv3 = wv.rearrange("p (c n) -> p c n", c=NCH)
                for s in range(4):
                    bk, bkey = bank_main()
                    for kc in range(NCH):
                        mm(bk[:], hT[:, kc, s * 128:(s + 1) * 128], wv3[:, kc, :], kc == 0, kc == NCH - 1,
                           [vkey, ("hT", kc)], [bkey])
                    k4, okey = r_ob.next()
                    ob = OB[:, k4 * TG:(k4 + 1) * TG]
                    act(ob, bk[:], AF.Copy, [bkey], [okey])
                    r0 = t * TG + s * 128
                    dma("pool", v_scr[r0:r0 + 128, vb * 512:(vb + 1) * 512], ob, [okey], [("v_scr", vb)], "st_v")

    def attn_tile(l, hh, qt, kT, qT, vh, hkey, nab, nkey, ost, ostkey):
        t0 = qt * 128
        if hh < 8:
            chunks = [(qt - 8 + j, j) for j in range(17) if qt - 8 + j >= 0]
        else:
            chunks = na_chunks(qt)
        numb, numkey = banks[2 + qt % 2], ("bank", 2 + qt % 2)
        denb, denkey = banks[4 + qt % 2], ("bank", 4 + qt % 2)
        ngrp = (len(chunks) + 3) // 4
        for g in range(ngrp):
            grp = chunks[g * 4:(g + 1) * 4]
            n = len(grp)
            sidx, _ = r_sc.next()
            scb, sckey = banks[sidx], ("bank", sidx)
            for i, (kc, ti) in enumerate(grp):
                mm(scb[:, i * 128:(i + 1) * 128], kT[:, kc * 128:(kc + 1) * 128], qT[:, t0:t0 + 128], True, True,
                   [hkey], [sckey])
            k4, pkey = r_ob.next()
            pt = OB[:, k4 * TG:k4 * TG + n * 128]
            k5, tkey = r_tmp.next()
            tmp = TMP[:, k5 * TG:k5 * TG + n * 128]
            ti0 = grp[0][1]
            if hh < 8:
                act(tmp, scb[:, 0:n * 128], AF.Exp, [sckey], [tkey])
                tt(pt, tmp, dmask[:, ti0 * 128:(ti0 + n) * 128], ALU.mult, [tkey, "dmask"], [pkey])
            else:
                tt(tmp, scb[:, 0:n * 128], nab[:, ti0 * 128:(ti0 + n) * 128], ALU.add, [sckey, nkey], [tkey])
                act(pt, tmp, AF.Exp, [tkey], [pkey])
            for i, (kc, ti) in enumerate(grp):
                first = (g == 0 and i == 0)
                last = (g == ngrp - 1 and i == n - 1)
                mm(numb[:, 0:128], vh[:, kc, :], pt[:, i * 128:(i + 1) * 128], first, last, [hkey, pkey], [numkey])
                mm(denb[:, 0:128], onesbf[:], pt[:, i * 128:(i + 1) * 128], first, last, ["onesbf", pkey], [denkey])
        k6, rckey = r_rc.next()
        rc = RC[:, k6 * 128:(k6 + 1) * 128]
        recip(rc, denb[:, 0:128], [denkey], [rckey])
        tt(ost[:, (qt % 4) * 128:(qt % 4 + 1) * 128], numb[:, 0:128], rc, ALU.mult, [numkey, rckey], [ostkey])

    def phase2(l, nq, nk):
        nkc = nk // 128
        HB = 18432
        for hh in range(NH):
            slot = hh % 2
            base = slot * HB
            kT = BIG[:, base:base + nk]
            qT = BIG[:, base + 6144:base + 6144 + nq]
            vh = BIG[:, base + 12288:base + 12288 + nk].rearrange("p (c d) -> p c d", d=128)
            hkey = ("head", slot)
            dma("sp", kT, kT_scr[hh][:, 0:nk], [], [hkey], hkey)
            dma("sp", qT, qT_scr[hh][:, 0:nq], [], [hkey], hkey)
            for c0 in range(0, nkc, 8):
                c1 = min(nkc, c0 + 8)
                dma("sp", vh[:, c0:c1, :],
                    v_scr[c0 * 128:c1 * 128, hh * 128:(hh + 1) * 128].rearrange("(c p) d -> p c d", p=128),
                    [], [hkey], hkey)
            nab, nkey = None, None
            if hh >= 8:
                nab = NAB[:, slot * 1664:(slot + 1) * 1664]
                nkey = ("nab", slot)
                dma("sp", nab, nab_in[l, hh - 8], [], [nkey], nkey)
            ost, ostkey = None, None
            for qt in range(nq // 128):
                if qt % 4 == 0:
                    k7, ostkey = r_ost.next()
                    ost = OST[:, k7 * TG:(k7 + 1) * TG]
                attn_tile(l, hh, qt, kT, qT, vh, hkey, nab, nkey, ost, ostkey)
                if qt % 4 == 3:
                    dma("pool", oT_scr[hh * 128:(hh + 1) * 128, (qt - 3) * 128:(qt + 1) * 128], ost,
                        [ostkey], [("oT_scr", hh)], "st_o")

    def phase3(l, xsrc, xdst, nq):
        hid = BIG[:, 0:NFC * TG].rearrange("p (c n) -> p c n", c=NFC)
        for t in range(nq // TG):
            tok = slice(t * TG, (t + 1) * TG)
            load_xg(xsrc, t)
            for c0 in range(0, NCH, 4):
                dma("sp", mix[:, c0:c0 + 4, :],
                    oT_scr[c0 * 128:(c0 + 4) * 128, tok].rearrange("(c p) n -> p c n", p=128),
                    [], [("hT", c) for c in range(c0, c0 + 4)], "mix")
            for oc in range(NCH):
                w, wkey = load_w(wo_bf[l, oc])
                bk, bkey = bank_main()
                for kc in range(NCH):
                    mm(bk[:], w[:, kc, :], mix[:, kc, :], kc == 0, kc == NCH - 1, [wkey, ("hT", kc)], [bkey])
                stt(xg[:, oc, :], bk[:], modT[l][:, 32 + oc:33 + oc], xg[:, oc, :], ALU.mult, ALU.add,
                    [bkey, ("xg", oc), ("modT", l)], [("xg", oc)])
            norm_group(l, 1)
            for fc in range(NFC):
                w, wkey = load_w(wmi_bf[l, fc])
                bk, bkey = bank_main()
                for kc in range(NCH):
                    mm(bk[:], w[:, kc, :], hT[:, kc, :], kc == 0, kc == NCH - 1, [wkey, ("hT", kc)], [bkey])
                k5, tkey = r_tmp.next()
                tmp = TMP[:, k5 * TG:(k5 + 1) * TG]
                act(tmp, bk[:], AF.Relu, [bkey], [tkey])
                tt(hid[:, fc, :], tmp, tmp, ALU.mult, [tkey], [("hid", fc)])
            for oc in range(NCH):
                k, vkey = r_wv.next()
                wm = WV[:, k * 8192:(k + 1) * 8192]
                dma("sp", wm, wmo_bf[l, oc], [], [vkey], vkey)
                wm3 = wm.rearrange("p (c n) -> p c n", c=NFC)
                bk, bkey = bank_main()
                for fc in range(NFC):
                    mm(bk[:], wm3[:, fc, :], hid[:, fc, :], fc == 0, fc == NFC - 1, [vkey, ("hid", fc)], [bkey])
                stt(xg[:, oc, :], bk[:], modT[l][:, 80 + oc:81 + oc], xg[:, oc, :], ALU.mult, ALU.add,
                    [bkey, ("xg", oc), ("modT", l)], [("xg", oc)])
            for c0 in range(0, NCH, 4):
                dma("pool", xdst[c0 * 128:(c0 + 4) * 128, tok].rearrange("(c p) n -> p c n", p=128),
                    xg[:, c0:c0 + 4, :], [("xg", c) for c in range(c0, c0 + 4)], [("xdst", c0)], "st_x")

    for l in range(depth):
        xsrc = xT_in if l == 0 else xa
        xdst = outT if l == depth - 1 else xa
        phase1(l, xsrc, NQ[l], NK[l])
        S.fence()
        phase2(l, NQ[l], NK[l])
        S.fence()
        phase3(l, xsrc, xdst, NQ[l])
        S.fence()

    with nc.Block() as block:
        S.emit_all(block, sem_of)
    return nc


def host_prep_shared(depth, w_ada, b_ada, ln1, ln2, w_in, qnd, knd, qnn, knn, w_out, w_mlp_in, w_mlp_out):
    sh = {}
    sh["w_ada_r"] = np.stack([relayout_w(w_ada[l], 128) for l in range(depth)])
    sh["w_ada_r"] = sh["w_ada_r"].reshape(depth, 96, 128, D)
    sh["b_adaT"] = np.stack([vecT(b_ada[l]) for l in range(depth)])
    sh["lnT"] = np.stack([np.stack([vecT(ln1[l]), vecT(ln2[l])]) for l in range(depth)])
    sh["gn"] = np.stack([np.stack([qnd[l], knd[l], qnn[l], knn[l]], axis=1) for l in range(depth)])
    wqk, wv = [], []
    for l in range(depth):
        w = w_in[l]
        cols = []
        vcols = []
        for hh in range(NH):
            if hh < 8:
                qb, kb, vb = hh * 128, 1024 + hh * 128, 2048 + hh * 128
            else:
                qb, kb, vb = 3072 + (hh - 8) * 128, 4096 + (hh - 8) * 128, 5120 + (hh - 8) * 128
            cols.append(w[:, qb:qb + 128])
            cols.append(w[:, kb:kb + 128])
            vcols.append(w[:, vb:vb + 128])
        wqk.append(relayout_w(np.concatenate(cols, axis=1), 128).reshape(32, 128, D))
        wv.append(relayout_w(np.concatenate(vcols, axis=1), 512).reshape(4, 128, NCH * 512))
    sh["wqk_r"] = np.stack(wqk)
    sh["wv_r"] = np.stack(wv)
    sh["wo_r"] = np.stack([relayout_w(w_out[l], 128).reshape(NCH, 128, D) for l in range(depth)])
    sh["wmi_r"] = np.stack([relayout_w(w_mlp_in[l], 128).reshape(NFC, 128, D) for l in range(depth)])
    sh["wmo_r"] = np.stack([relayout_w(w_mlp_out[l], 128).reshape(NCH, 128, NFC * 128) for l in range(depth)])
    sh["dmask"] = dil_mask_table()
    sh["rotm"] = rot_matrix()
    return {k: np.ascontiguousarray(v) for k, v in sh.items()}


def host_prep_core(depth, S, NX, xb, cb, rev, na_rel_bias):
    m = {}
    xs = xb[::-1][:NX] if rev else xb[:NX]
    m["xT"] = np.ascontiguousarray(xs.T)
    m["cT"] = vecT(cb)
    allowed, roff, coff = na_index_tables(S, rev)
    nab = np.empty((depth, 8, 128, 13 * 128), np.float32)
    for l in range(depth):
        for h in range(8):
            g = na_rel_bias[l, h][roff, coff]
            g = np.where(allowed, g, np.float32(NEG))
            nab[l, h] = g.transpose(1, 0, 2).reshape(128, 13 * 128)
    m["nabias"] = nab
    cos, sin = rope_tables(S, rev, NX)
    m["cosT"], m["sinT"] = cos, sin
    return m


_PROG = {}


def run_model(depth, S, NQF, x, c, shared, na_rel_bias, cores):
    key = (depth, NQF)
    if key not in _PROG:
        _PROG[key] = build_program(depth, NQF)
    nc = _PROG[key]
    NX = NQF + 1024 * depth
    in_maps = []
    for (b, h) in cores:
        m = dict(shared)
        m.update(host_prep_core(depth, S, NX, x[b], c[b], h == 1, na_rel_bias))
        in_maps.append(m)
    res = run_bass_kernel_spmd(nc, in_maps, core_ids=list(range(len(cores))))
    outs = []
    for (b, h), r in zip(cores, res.results):
        o = r["outT"].T
        outs.append(o[::-1] if h == 1 else o)
    return outs


def kernel(x, c, ln1, w_ada, b_ada, w_in, q_norm_dil, k_norm_dil, q_norm_na, k_norm_na,
           na_rel_bias, w_out, ln2, w_mlp_in, w_mlp_out):
    f = lambda a: np.asarray(a, dtype=np.float32)
    x, c = f(x), f(c)
    depth = 2
    B, S, _ = x.shape
    shared = host_prep_shared(depth, f(w_ada), f(b_ada), f(ln1), f(ln2), f(w_in), f(q_norm_dil), f(k_norm_dil),
                              f(q_norm_na), f(k_norm_na), f(w_out), f(w_mlp_in), f(w_mlp_out))
    cores = [(b, h) for b in range(B) for h in range(2)]
    outs = run_model(depth, S, S // 2, x, c, shared, f(na_rel_bias), cores)
    out = np.empty((B, S, D), np.float32)
    for (b, h), o in zip(cores, outs):
        if h == 0:
            out[b, :S // 2] = o
        else:
            out[b, S // 2:] = o
    return out
```

```python
import numpy as np
import ml_dtypes
import concourse.bass as bass
import concourse.mybir as mybir
from concourse.bass_utils import run_bass_kernel_spmd

F32 = mybir.dt.float32
BF16 = mybir.dt.bfloat16
ALU = mybir.AluOpType
AF = mybir.ActivationFunctionType

D = 2048
NCH = 16
DFF = 8192
NFC = 64
HD = 128
NH = 16
EPS = 1e-6
SCALE = HD ** -0.5
TG = 512
NEG = -30000.0
ENGS = ("pe", "act", "dve", "pool", "sp")


class Op:
    __slots__ = ("eng", "emit", "chan", "deps", "signal", "ticket")

    def __init__(self, eng, emit, chan):
        self.eng = eng
        self.emit = emit
        self.chan = chan
        self.deps = []
        self.signal = False
        self.ticket = 0


class Sched:
    def __init__(self, nc):
        self.nc = nc
        self.ops = {e: [] for e in ENGS}
        self.last_writer = {}
        self.readers = {}
        self.chan_count = {}
        self.sync_same = {"act", "dve", "pool"}
        self.pending = {e: [] for e in ENGS}

    def add(self, eng, emit, reads=(), writes=(), chan=None):
        op = Op(eng, emit, chan)
        deps = {}
        lw = self.last_writer
        rd = self.readers
        for b in reads:
            w = lw.get(b)
            if w is not None:
                deps[id(w)] = w
        for b in writes:
            w = lw.get(b)
            if w is not None:
                if not (chan is not None and w.chan == chan and not rd.get(b)):
                    deps[id(w)] = w
            for r in rd.get(b, {}).values():
                deps[id(r)] = r
        dl = list(self.pending[eng])
        self.pending[eng] = []
        for w in deps.values():
            if w.chan is None:
                if w.eng == eng and eng not in self.sync_same:
                    continue
                w.signal = True
                dl.append(w)
            else:
                dl.append((("c", w.chan), self.chan_count[w.chan]))
        op.deps = dl
        if chan is not None:
            self.chan_count[chan] = self.chan_count.get(chan, 0) + 16
            op.ticket = self.chan_count[chan]
        rk = (eng, chan)
        for b in reads:
            rd.setdefault(b, {})[rk] = op
        for b in writes:
            lw[b] = op
            rd[b] = {}
        self.ops[eng].append(op)
        return op

    def fence(self):
        deps = []
        for e in ENGS:
            for op in reversed(self.ops[e]):
                if op.chan is None:
                    op.signal = True
                    deps.append(op)
                    break
        for c, v in self.chan_count.items():
            deps.append((("c", c), v))
        for e in ENGS:
            self.pending[e] = list(deps)
        self.last_writer = {}
        self.readers = {}

    def emit_all(self, block, sem_of):
        for e in ENGS:
            c = 0
            for op in self.ops[e]:
                if op.chan is None and op.signal:
                    c += 1
                    op.ticket = c

        def run(engname, eng):
            seen = {}
            for op in self.ops[engname]:
                for d in op.deps:
                    if isinstance(d, Op):
                        if d.eng == engname and engname == "pe":
                            continue
                        key, val = ("e", d.eng), d.ticket
                    else:
                        key, val = d
                    if seen.get(key, 0) >= val:
                        continue
                    seen[key] = val
                    eng.wait_ge(sem_of(key), val)
                ins = op.emit(eng)
                if op.chan is not None:
                    ins.then_inc(sem_of(("c", op.chan)), 16)
                elif op.signal:
                    ins.then_inc(sem_of(("e", engname)), 1)
            for c, v in self.chan_count.items():
                key = ("c", c)
                if seen.get(key, 0) < v:
                    seen[key] = v
                    eng.wait_ge(sem_of(key), v)

        @block.tensor
        def _(e):
            run("pe", e)

        @block.scalar
        def _(e):
            run("act", e)

        @block.vector
        def _(e):
            run("dve", e)

        @block.gpsimd
        def _(e):
            run("pool", e)

        @block.sync
        def _(e):
            run("sp", e)


def dil_mask_table():
    k = np.arange(128)[:, None]
    q = np.arange(128)[None, :]
    out = np.zeros((128, 17, 128), np.float32)
    for j in range(17):
        dlt = -1024 + 128 * j + k - q
        a = np.abs(dlt)
        c = (a <= 64).astype(np.float32) + ((dlt % 4 == 0) & (a <= 256)) + ((dlt % 16 == 0) & (a <= 1024))
        out[:, j, :] = c
    out2 = np.zeros((128, 23, 128), np.float32)
    for pp in range(23):
        j = 19 - pp
        if 0 <= j <= 16:
            out2[:, pp, :] = out[:, j, :]
    return out2.reshape(128, 23 * 128).astype(ml_dtypes.bfloat16)


NA_TILES = ([(10, 10 + o) for o in (-2, -1, 0, 1, 2)] + [(0, c) for c in (0, 1, 2, 3)] + [(1, c) for c in (0, 1, 2, 3)])


def na_chunks(qt):
    if qt == 0:
        return [(c, 5 + c) for c in range(4)]
    if qt == 1:
        return [(c, 9 + c) for c in range(4)]
    return [(qt + o, 2 + o) for o in (-2, -1, 0, 1, 2)]


def na_index_tables(S, rev):
    rows = S // 64
    allowed = np.zeros((13, 128, 128), bool)
    roff = np.zeros((13, 128, 128), np.int64)
    coff = np.zeros((13, 128, 128), np.int64)
    loc = np.arange(128)
    for ti, (qt, kc) in enumerate(NA_TILES):
        ql = qt * 128 + loc
        kl = kc * 128 + loc
        qg = (S - 1 - ql) if rev else ql
        kg = (S - 1 - kl) if rev else kl
        qi, qc = qg // 64, qg % 64
        ki, kcc = kg // 64, kg % 64
        rs = np.clip(qi - 4, 0, rows - 8)
        cs = np.clip(qc - 8, 0, 64 - 16)
        al = ((ki[:, None] >= rs[None, :]) & (ki[:, None] < rs[None, :] + 8) &
              (kcc[:, None] >= cs[None, :]) & (kcc[:, None] < cs[None, :] + 16))
        ro = ki[:, None] - qi[None, :] + 7
        co = np.clip(kcc[:, None] - qc[None, :] + 15, 0, 30)
        allowed[ti] = al
        roff[ti] = np.clip(ro, 0, 14)
        coff[ti] = co
    return allowed, roff, coff


def rope_tables(S, rev, n):
    l = np.arange(n)
    pos = ((S - 1 - l) if rev else l).astype(np.float32)
    inv = (np.float32(500000.0) ** (-np.arange(0, 32, 2, dtype=np.float32) / np.float32(32))).astype(np.float32)
    ang = pos[None, :] * inv[:, None]
    cos = np.cos(ang).astype(np.float32)
    sin = np.sin(ang).astype(np.float32)
    return np.concatenate([cos, cos], 0), np.concatenate([sin, sin], 0)


def rot_matrix():
    R = np.zeros((32, 32), np.float32)
    for dp in range(16):
        R[dp + 16, dp] = -1.0
        R[dp, dp + 16] = 1.0
    return R


def relayout_w(w, ncol_chunk):
    K, N = w.shape
    return np.ascontiguousarray(w.reshape(K // 128, 128, N // ncol_chunk, ncol_chunk).transpose(2, 1, 0, 3))


def vecT(v):
    return np.ascontiguousarray(v.reshape(-1, 128).T)


def build_program(depth, NQF, debug=False):
    NQ = [NQF + 1024 * (depth - 1 - l) for l in range(depth)]
    NK = [q + 1024 for q in NQ]
    NX = NK[0]
    nc = bass.Bass("TRN2", target_bir_lowering=False)
    S = Sched(nc)
    sems = {}

    def sem_of(key):
        if key not in sems:
            sems[key] = nc.alloc_semaphore("s%d" % len(sems))
        return sems[key]

    def din(name, shape, dt=F32):
        return nc.dram_tensor(name, list(shape), dt, kind="ExternalInput").ap()

    def dscr(name, shape, dt):
        return nc.dram_tensor(name, list(shape), dt).ap()

    xT_in = din("xT", [D, NX])
    cT_in = din("cT", [128, NCH])
    wada_in = din("w_ada_r", [depth, 96, 128, D])
    badaT_in = din("b_adaT", [depth, 128, 96])
    lnT_in = din("lnT", [depth, 2, 128, NCH])
    gn_in = din("gn", [depth, 128, 4])
    wqk_in = din("wqk_r", [depth, 32, 128, D])
    wv_in = din("wv_r", [depth, 4, 128, NCH * 512])
    wo_in = din("wo_r", [depth, NCH, 128, D])
    wmi_in = din("wmi_r", [depth, NFC, 128, D])
    wmo_in = din("wmo_r", [depth, NCH, 128, NFC * 128])
    nab_in = din("nabias", [depth, 8, 128, 35 * 128])
    dmask_in = din("dmask", [128, 23 * 128], BF16)
    cos_in = din("cosT", [32, NX])
    sin_in = din("sinT", [32, NX])
    rot_in = din("rotm", [32, 32])
    outT = nc.dram_tensor("outT", [D, NQF], F32, kind="ExternalOutput").ap()

    wqk_bf = dscr("wqk_bf", [depth, 32, 128, D], BF16)
    wv_bf = dscr("wv_bf", [depth, 4, 128, NCH * 512], BF16)
    wo_bf = dscr("wo_bf", [depth, NCH, 128, D], BF16)
    wmi_bf = dscr("wmi_bf", [depth, NFC, 128, D], BF16)
    wmo_bf = dscr("wmo_bf", [depth, NCH, 128, NFC * 128], BF16)
    qT_scr = dscr("qT_scr", [NH, 128, NX], BF16)
    kT_scr = dscr("kT_scr", [NH, 128, NX], BF16)
    v_scr = dscr("v_scr", [NX, D], BF16)
    oT_scr = dscr("oT_scr", [D, NQ[0]], BF16)
    xa = dscr("xa", [D, NQ[0]], F32)

    def sb(name, shape, dt):
        return nc.alloc_sbuf_tensor(name, list(shape), dt)

    onesbf = sb("onesbf", [128, 128], BF16)
    cact = sb("cact", [128, NCH], F32)
    modT = [sb("modT%d" % l, [128, 96], F32) for l in range(depth)]
    lnT = [sb("lnT%d" % l, [128, 2 * NCH], F32) for l in range(depth)]
    G = [sb("G%d" % l, [128, 2 * NCH], F32) for l in range(depth)]
    gn = [sb("gn%d" % l, [128, 4], F32) for l in range(depth)]
    rotm = sb("rotm_s", [32, 32], F32)
    dmask = sb("dmask_s", [128, 23 * 128], BF16)

    A32 = sb("A32", [128, NCH * TG], F32)
    B16 = sb("B16", [128, NCH * TG], BF16)
    BIG = sb("BIG", [128, 36864], BF16)
    WR = sb("WR", [128, 4 * 2048], BF16)
    WV = sb("WV", [128, 2 * 8192], BF16)
    SQ = sb("SQ", [128, 2 * TG], BF16)
    RS = sb("RS", [128, 2 * TG], F32)
    TMP = sb("TMP", [128, 4 * TG], F32)
    OB = sb("OB", [128, 8 * TG], BF16)
    OST = sb("OST", [128, 2 * TG], BF16)
    CS = sb("CS", [32, 2 * TG], F32)
    NAB = A32[:, 0:35 * 128]
    RC = A32[:, 4608:4608 + 2 * TG]

    banks = [nc.alloc_psum_tensor("bank%d" % i, [128, TG], F32) for i in range(8)]

    xg = A32[:].rearrange("p (c n) -> p c n", c=NCH)
    hT = B16[:].rearrange("p (c n) -> p c n", c=NCH)
    mix = hT

    class Ring:
        def __init__(self, name, n):
            self.name, self.n, self.i = name, n, 0

        def next(self):
            k = self.i % self.n
            self.i += 1
            return k, (self.name, k)

    r_wr, r_wv, r_sq, r_rs, r_tmp, r_ob, r_rc = (Ring("WR", 4), Ring("WV", 2), Ring("SQ", 2), Ring("RS", 2),
                                                 Ring("TMP", 4), Ring("OB", 8), Ring("RC", 2))
    r_bank = Ring("bank", 4)
    r_st2 = Ring("bank2", 2)
    r_sc = Ring("sc", 4)
    r_ost = Ring("OST", 2)
    r_ada = Ring("ADA", 2)

    def bank_main():
        k, _ = r_bank.next()
        return banks[k], ("bank", k)

    def dma(eng, out, in_, reads, writes, chan):
        S.add(eng, lambda e: e.dma_start(out=out, in_=in_), reads=reads, writes=writes, chan=chan)

    def mm(out, lhsT, rhs, start, stop, reads, writes):
        S.add("pe", lambda e: e.matmul(out, lhsT, rhs, start=start, stop=stop), reads=reads, writes=writes)

    def act(out, in_, func, reads, writes, bias=None, scale=None):
        kw = {}
        if bias is not None:
            kw["bias"] = bias
        if scale is not None:
            kw["scale"] = scale
        S.add("act", lambda e: e.activation(out=out, in_=in_, func=func, **kw), reads=reads, writes=writes)

    def tt(out, in0, in1, op, reads, writes, eng="dve"):
        S.add(eng, lambda e: e.tensor_tensor(out=out, in0=in0, in1=in1, op=op), reads=reads, writes=writes)

    def stt(out, in0, scalar, in1, op0, op1, reads, writes, eng="dve"):
        S.add(eng, lambda e: e.scalar_tensor_tensor(out=out, in0=in0, scalar=scalar, in1=in1, op0=op0, op1=op1),
              reads=reads, writes=writes)

    def ts(out, in0, s1, s2, op0, op1, reads, writes, eng="dve"):
        S.add(eng, lambda e: e.tensor_scalar(out=out, in0=in0, scalar1=s1, scalar2=s2, op0=op0, op1=op1),
              reads=reads, writes=writes)

    def recip(out, in_, reads, writes):
        S.add("dve", lambda e: e.reciprocal(out=out, in_=in_), reads=reads, writes=writes)

    def memset(t, v, key):
        S.add("dve", lambda e: e.memset(t, v), writes=[key])

    memset(onesbf[:], 1.0, "onesbf")
    dma("sp", cact[:], cT_in, [], ["cact"], "const")
    dma("sp", rotm[:], rot_in, [], ["rotm"], "const")
    dma("sp", dmask[:], dmask_in, [], ["dmask"], "const")
    for l in range(depth):
        dma("sp", lnT[l][:, 0:NCH], lnT_in[l, 0], [], [("lnT", l)], "const")
        dma("sp", lnT[l][:, NCH:2 * NCH], lnT_in[l, 1], [], [("lnT", l)], "const")
        dma("sp", gn[l][:], gn_in[l], [], [("gn", l)], "const")
        dma("sp", modT[l][:], badaT_in[l], [], [("modT", l)], "const")

    wc_i = [0]

    def cast_copy(dst, src, rows_per=512):
        d2 = dst.rearrange("a p n -> (a p) n")
        s2 = src.rearrange("a p n -> (a p) n")
        R, C = d2.shape
        if C > 2048:
            d2 = d2.rearrange("r (a n) -> (r a) n", n=2048)
            s2 = s2.rearrange("r (a n) -> (r a) n", n=2048)
            R = d2.shape[0]
        for r0 in range(0, R, rows_per):
            r1 = min(R, r0 + rows_per)
            i = wc_i[0]
            wc_i[0] += 1
            dma("pool", d2[r0:r1, :], s2[r0:r1, :], [("wc", (i - 2) % 4)] if i >= 2 else [], [("wc", i % 4)],
                ("wc", i % 4))

    for l in range(depth):
        cast_copy(wqk_bf[l], wqk_in[l])
        cast_copy(wv_bf[l], wv_in[l])
        cast_copy(wo_bf[l], wo_in[l])
        cast_copy(wmi_bf[l], wmi_in[l])
        cast_copy(wmo_bf[l], wmo_in[l])

    act(cact[:], cact[:], AF.Silu, ["cact"], ["cact"])
    adar = A32[:].rearrange("p (s j n) -> p s j n", s=2, j=2)
    mps = banks[4]
    for l in range(depth):
        for j0 in range(0, 96, 2):
            k, key = r_ada.next()
            dma("sp", adar[:, k], wada_in[l, j0:j0 + 2].rearrange("j p n -> p j n"), [], [key], key)
            for jj in range(2):
                j = j0 + jj
                for kc in range(NCH):
                    mm(mps[:, j:j + 1], adar[:, k, jj, kc * 128:(kc + 1) * 128], cact[:, kc:kc + 1],
                       kc == 0, kc == NCH - 1, [key, "cact"], [("bank", 4)])
        tt(modT[l][:], mps[:, 0:96], modT[l][:], ALU.add, [("bank", 4), ("modT", l)], [("modT", l)])
        stt(G[l][:, 0:NCH], modT[l][:, 16:32], 1.0, lnT[l][:, 0:NCH], ALU.add, ALU.mult,
            [("modT", l), ("lnT", l)], [("G", l)])
        stt(G[l][:, NCH:2 * NCH], modT[l][:, 64:80], 1.0, lnT[l][:, NCH:2 * NCH], ALU.add, ALU.mult,
            [("modT", l), ("lnT", l)], [("G", l)])
        ts(gn[l][:, 0:1], gn[l][:, 0:1], SCALE, 0.0, ALU.mult, ALU.add, [("gn", l)], [("gn", l)])
        ts(gn[l][:, 2:3], gn[l][:, 2:3], SCALE, 0.0, ALU.mult, ALU.add, [("gn", l)], [("gn", l)])
    S.fence()

    def norm_group(l, which):
        st = banks[4]
        for c in range(NCH):
            k, key = r_sq.next()
            sq = SQ[:, k * TG:(k + 1) * TG]
            act(sq, xg[:, c, :], AF.Square, [("xg", c)], [key])
            mm(st[:], onesbf[:], sq, c == 0, c == NCH - 1, [key, "onesbf"], [("bank", 4)])
        k, rkey = r_rs.next()
        rs = RS[:, k * TG:(k + 1) * TG]
        act(rs, st[:], AF.Ln, [("bank", 4)], [rkey], bias=EPS, scale=1.0 / D)
        act(rs, rs, AF.Exp, [rkey], [rkey], scale=-0.5)
        gcol = which * NCH
        shcol = 0 if which == 0 else 48
        for c in range(NCH):
            k2, tkey = r_tmp.next()
            tmp = TMP[:, k2 * TG:(k2 + 1) * TG]
            tt(tmp, xg[:, c, :], rs, ALU.mult, [("xg", c), rkey], [tkey], eng="pool")
            ts(hT[:, c, :], tmp, G[l][:, gcol + c:gcol + c + 1], modT[l][:, shcol + c:shcol + c + 1],
               ALU.mult, ALU.add, [tkey, ("G", l), ("modT", l)], [("hT", c)])

    def load_xg(src, t):
        for c0 in range(0, NCH, 4):
            dma("sp", xg[:, c0:c0 + 4, :],
                src[c0 * 128:(c0 + 4) * 128, t * TG:(t + 1) * TG].rearrange("(c p) n -> p c n", p=128),
                [], [("xg", c) for c in range(c0, c0 + 4)], "xg")

    def load_w(dram_chunk):
        k, key = r_wr.next()
        dma("sp", WR[:, k * 2048:(k + 1) * 2048], dram_chunk, [], [key], key)
        return WR[:, k * 2048:(k + 1) * 2048].rearrange("p (c n) -> p c n", c=NCH), key

    def qk_post(l, hh, qk, bk, bkey, tok):
        k, skey = r_sq.next()
        sq = SQ[:, k * TG:(k + 1) * TG]
        act(sq, bk[:], AF.Square, [bkey], [skey])
        k2, _ = r_st2.next()
        st2, st2key = banks[5 + k2], ("bank", 5 + k2)
        mm(st2[:], onesbf[:], sq, True, True, [skey, "onesbf"], [st2key])
        k3, rkey = r_rs.next()
        rs = RS[:, k3 * TG:(k3 + 1) * TG]
        act(rs, st2[:], AF.Ln, [st2key], [rkey], bias=EPS, scale=1.0 / HD)
        act(rs, rs, AF.Exp, [rkey], [rkey], scale=-0.5)
        gcol = (0 if hh < 8 else 2) + qk
        gsc = gn[l][:, gcol:gcol + 1]
        k4, okey = r_ob.next()
        ob = OB[:, k4 * TG:(k4 + 1) * TG]
        if hh >= 8:
            stt(ob, bk[:], gsc, rs, ALU.mult, ALU.mult, [bkey, rkey, ("gn", l)], [okey])
        else:
            k5, tkey = r_tmp.next()
            qn = TMP[:, k5 * TG:(k5 + 1) * TG]
            stt(qn, bk[:], gsc, rs, ALU.mult, ALU.mult, [bkey, rkey, ("gn", l)], [tkey])
            mm(banks[7][0:32, :], rotm[:], qn[0:32, :], True, True, [tkey, "rotm"], [("bank", 7)])
            k6, t2key = r_tmp.next()
            t1 = TMP[0:32, k6 * TG:(k6 + 1) * TG]
            tt(t1, banks[7][0:32, :], CS[:, TG:2 * TG], ALU.mult, [("bank", 7), "sin"], [t2key])
            tt(qn[0:32, :], qn[0:32, :], CS[:, 0:TG], ALU.mult, [tkey, "cos"], [tkey], eng="pool")
            tt(qn[0:32, :], qn[0:32, :], t1, ALU.add, [tkey, t2key], [tkey], eng="pool")
            S.add("act", lambda e: e.copy(ob, qn), reads=[tkey], writes=[okey])
        dst = (qT_scr if qk == 0 else kT_scr)[hh][:, tok]
        dma("pool", dst, ob, [okey], [("qk_scr", qk, hh)], "st_qk")

    def phase1(l, xsrc, nq, nk):
        for t in range(nk // TG):
            tok = slice(t * TG, (t + 1) * TG)
            load_xg(xsrc, t)
            dma("sp", CS[:, 0:TG], cos_in[:, tok], [], ["cos"], "cs")
            dma("sp", CS[:, TG:2 * TG], sin_in[:, tok], [], ["sin"], "cs")
            norm_group(l, 0)
            for hh in range(NH):
                for qk in range(2):
                    if qk == 0 and t * TG >= nq:
                        continue
                    w, wkey = load_w(wqk_bf[l, 2 * hh + qk])
                    bk, bkey = bank_main()
                    for kc in range(NCH):
                        mm(bk[:], w[:, kc, :], hT[:, kc, :], kc == 0, kc == NCH - 1, [wkey, ("hT", kc)], [bkey])
                    qk_post(l, hh, qk, bk, bkey, tok)
            for vb in range(4):
                k, vkey = r_wv.next()
                wv = WV[:, k * 8192:(k + 1) * 8192]
                dma("sp", wv, wv_bf[l, vb], [], [vkey], vkey)
                wv3 = wv.rearrange("p (c n) -> p c n", c=NCH)
                for s in range(4):
                    bk, bkey = bank_main()
                    for kc in range(NCH):
                        mm(bk[:], hT[:, kc, s * 128:(s + 1) * 128], wv3[:, kc, :], kc == 0, kc == NCH - 1,
                           [vkey, ("hT", kc)], [bkey])
                    k4, okey = r_ob.next()
                    ob = OB[:, k4 * TG:(k4 + 1) * TG]
                    act(ob, bk[:], AF.Copy, [bkey], [okey])
                    r0 = t * TG + s * 128
                    dma("pool", v_scr[r0:r0 + 128, vb * 512:(vb + 1) * 512], ob, [okey], [("v_scr", vb)], "st_v")

    def phase2(l, nq, nk):
        HB = 18432
        nkc = nk // 128
        for hh in range(NH):
            slot = hh % 2
            base = slot * HB
            kT = BIG[:, base:base + nk]
            qT = BIG[:, base + 6144:base + 6144 + nq]
            vh = BIG[:, base + 12288:base + 12288 + nk].rearrange("p (c d) -> p c d", d=128)
            hkey = ("head", slot)
            dma("sp", kT, kT_scr[hh][:, 0:nk], [], [hkey], hkey)
            dma("sp", qT, qT_scr[hh][:, 0:nq], [], [hkey], hkey)
            for c0 in range(0, nkc, 8):
                c1 = min(nkc, c0 + 8)
                dma("sp", vh[:, c0:c1, :],
                    v_scr[c0 * 128:c1 * 128, hh * 128:(hh + 1) * 128].rearrange("(c p) d -> p c d", p=128),
                    [], [hkey], hkey)
            if hh >= 8:
                dma("sp", NAB, nab_in[l, hh - 8], [], ["nab"], "nab")
            for qg in range(nq // TG):
                qt0 = 4 * qg
                if hh < 8:
                    kcs = list(range(max(0, qt0 - 8), qt0 + 12))
                else:
                    kcs = list(range(max(0, qt0 - 2), qt0 + 6))
                numb, numkey = banks[4 + qg % 2], ("bank", 4 + qg % 2)
                denb, denkey = banks[6 + qg % 2], ("bank", 6 + qg % 2)
                qs = qT[:, qg * TG:(qg + 1) * TG]
                for idx, kc in enumerate(kcs):
                    sidx, _ = r_sc.next()
                    scb, sckey = banks[sidx], ("bank", sidx)
                    mm(scb[:], kT[:, kc * 128:(kc + 1) * 128], qs, True, True, [hkey], [sckey])
                    k4, pkey = r_ob.next()
                    pt = OB[:, k4 * TG:(k4 + 1) * TG]
                    if hh < 8:
                        k5, tkey = r_ob.next()
                        tmpb = OB[:, k5 * TG:(k5 + 1) * TG]
                        pp0 = 11 - kc + qt0
                        act(tmpb, scb[:], AF.Exp, [sckey], [tkey])
                        tt(pt, tmpb, dmask[:, pp0 * 128:(pp0 + 4) * 128], ALU.mult, [tkey, "dmask"], [pkey])
                    else:
                        k5, tkey = r_tmp.next()
                        tmp = TMP[:, k5 * TG:(k5 + 1) * TG]
                        if qg == 0:
                            bias = NAB[:, (11 + 4 * kc) * 128:(11 + 4 * kc + 4) * 128]
                        else:
                            pp0 = 5 - kc + qt0
                            bias = NAB[:, pp0 * 128:(pp0 + 4) * 128]
                        tt(tmp, scb[:], bias, ALU.add, [sckey, "nab"], [tkey])
                        act(pt, tmp, AF.Exp, [tkey], [pkey])
                    first, last = idx == 0, idx == len(kcs) - 1
                    mm(numb[:], vh[:, kc, :], pt, first, last, [hkey, pkey], [numkey])
                    mm(denb[:], onesbf[:], pt, first, last, ["onesbf", pkey], [denkey])
                k6, rckey = r_rc.next()
                rc = RC[:, k6 * TG:(k6 + 1) * TG]
                recip(rc, denb[:], [denkey], [rckey])
                k7, ostkey = r_ost.next()
                ost = OST[:, k7 * TG:(k7 + 1) * TG]
                tt(ost, numb[:], rc, ALU.mult, [numkey, rckey], [ostkey])
                dma("pool", oT_scr[hh * 128:(hh + 1) * 128, qg * TG:(qg + 1) * TG], ost,
                    [ostkey], [("oT_scr", hh)], "st_o")

    def phase3(l, xsrc, xdst, nq):
        hid = BIG[:, 0:NFC * TG].rearrange("p (c n) -> p c n", c=NFC)
        for t in range(nq // TG):
            tok = slice(t * TG, (t + 1) * TG)
            load_xg(xsrc, t)
            for c0 in range(0, NCH, 4):
                dma("sp", mix[:, c0:c0 + 4, :],
                    oT_scr[c0 * 128:(c0 + 4) * 128, tok].rearrange("(c p) n -> p c n", p=128),
                    [], [("hT", c) for c in range(c0, c0 + 4)], "mix")
            for oc in range(NCH):
                w, wkey = load_w(wo_bf[l, oc])
                bk, bkey = bank_main()
                for kc in range(NCH):
                    mm(bk[:], w[:, kc, :], mix[:, kc, :], kc == 0, kc == NCH - 1, [wkey, ("hT", kc)], [bkey])
                stt(xg[:, oc, :], bk[:], modT[l][:, 32 + oc:33 + oc], xg[:, oc, :], ALU.mult, ALU.add,
                    [bkey, ("xg", oc), ("modT", l)], [("xg", oc)])
            norm_group(l, 1)
            for fc in range(NFC):
                w, wkey = load_w(wmi_bf[l, fc])
                bk, bkey = bank_main()
                for kc in range(NCH):
                    mm(bk[:], w[:, kc, :], hT[:, kc, :], kc == 0, kc == NCH - 1, [wkey, ("hT", kc)], [bkey])
                k5, tkey = r_tmp.next()
                tmp = TMP[:, k5 * TG:(k5 + 1) * TG]
                act(tmp, bk[:], AF.Relu, [bkey], [tkey])
                tt(hid[:, fc, :], tmp, tmp, ALU.mult, [tkey], [("hid", fc)])
            for oc in range(NCH):
                k, vkey = r_wv.next()
                wm = WV[:, k * 8192:(k + 1) * 8192]
                dma("sp", wm, wmo_bf[l, oc], [], [vkey], vkey)
                wm3 = wm.rearrange("p (c n) -> p c n", c=NFC)
                bk, bkey = bank_main()
                for fc in range(NFC):
                    mm(bk[:], wm3[:, fc, :], hid[:, fc, :], fc == 0, fc == NFC - 1, [vkey, ("hid", fc)], [bkey])
                stt(xg[:, oc, :], bk[:], modT[l][:, 80 + oc:81 + oc], xg[:, oc, :], ALU.mult, ALU.add,
                    [bkey, ("xg", oc), ("modT", l)], [("xg", oc)])
            for c0 in range(0, NCH, 4):
                dma("pool", xdst[c0 * 128:(c0 + 4) * 128, tok].rearrange("(c p) n -> p c n", p=128),
                    xg[:, c0:c0 + 4, :], [("xg", c) for c in range(c0, c0 + 4)], [("xdst", c0)], "st_x")

    for l in range(depth):
        xsrc = xT_in if l == 0 else xa
        xdst = outT if l == depth - 1 else xa
        phase1(l, xsrc, NQ[l], NK[l])
        S.fence()
        phase2(l, NQ[l], NK[l])
        S.fence()
        phase3(l, xsrc, xdst, NQ[l])
        S.fence()

    with nc.Block() as block:
        S.emit_all(block, sem_of)
    return nc


def host_prep_shared(depth, w_ada, b_ada, ln1, ln2, w_in, qnd, knd, qnn, knn, w_out, w_mlp_in, w_mlp_out):
    sh = {}
    sh["w_ada_r"] = np.stack([relayout_w(w_ada[l], 128) for l in range(depth)])
    sh["w_ada_r"] = sh["w_ada_r"].reshape(depth, 96, 128, D)
    sh["b_adaT"] = np.stack([vecT(b_ada[l]) for l in range(depth)])
    sh["lnT"] = np.stack([np.stack([vecT(ln1[l]), vecT(ln2[l])]) for l in range(depth)])
    sh["gn"] = np.stack([np.stack([qnd[l], knd[l], qnn[l], knn[l]], axis=1) for l in range(depth)])
    wqk, wv = [], []
    for l in range(depth):
        w = w_in[l]
        cols = []
        vcols = []
        for hh in range(NH):
            if hh < 8:
                qb, kb, vb = hh * 128, 1024 + hh * 128, 2048 + hh * 128
            else:
                qb, kb, vb = 3072 + (hh - 8) * 128, 4096 + (hh - 8) * 128, 5120 + (hh - 8) * 128
            cols.append(w[:, qb:qb + 128])
            cols.append(w[:, kb:kb + 128])
            vcols.append(w[:, vb:vb + 128])
        wqk.append(relayout_w(np.concatenate(cols, axis=1), 128).reshape(32, 128, D))
        wv.append(relayout_w(np.concatenate(vcols, axis=1), 512).reshape(4, 128, NCH * 512))
    sh["wqk_r"] = np.stack(wqk)
    sh["wv_r"] = np.stack(wv)
    sh["wo_r"] = np.stack([relayout_w(w_out[l], 128).reshape(NCH, 128, D) for l in range(depth)])
    sh["wmi_r"] = np.stack([relayout_w(w_mlp_in[l], 128).reshape(NFC, 128, D) for l in range(depth)])
    sh["wmo_r"] = np.stack([relayout_w(w_mlp_out[l], 128).reshape(NCH, 128, NFC * 128) for l in range(depth)])
    sh["dmask"] = dil_mask_table()
    sh["rotm"] = rot_matrix()
    return {k: np.ascontiguousarray(v) for k, v in sh.items()}


def host_prep_core(depth, S, NX, xb, cb, rev, na_rel_bias):
    m = {}
    xs = xb[::-1][:NX] if rev else xb[:NX]
    m["xT"] = np.ascontiguousarray(xs.T)
    m["cT"] = vecT(cb)
    allowed, roff, coff = na_index_tables(S, rev)
    nab = np.empty((depth, 8, 128, 35 * 128), np.float32)
    for l in range(depth):
        for h in range(8):
            g = na_rel_bias[l, h][roff, coff]
            g = np.where(allowed, g, np.float32(NEG))
            negt = np.full((128, 128), NEG, np.float32)
            tiles = []
            for pp in range(11):
                o = 5 - pp
                tiles.append(g[2 + o] if abs(o) <= 2 else negt)
            for kc in range(6):
                tiles.append(g[5 + kc] if kc <= 3 else negt)
                tiles.append(g[9 + kc] if kc <= 3 else negt)
                tiles.append(g[kc] if abs(kc - 2) <= 2 else negt)
                tiles.append(g[kc - 1] if abs(kc - 3) <= 2 else negt)
            nab[l, h] = np.stack(tiles, axis=1).reshape(128, 35 * 128)
    m["nabias"] = nab
    cos, sin = rope_tables(S, rev, NX)
    m["cosT"], m["sinT"] = cos, sin
    return m


_PROG = {}
_RUN_KW = {}
_LAST = {}


def run_model(depth, S, NQF, x, c, shared, na_rel_bias, cores):
    key = (depth, NQF)
    if key not in _PROG:
        _PROG[key] = build_program(depth, NQF)
    nc = _PROG[key]
    NX = NQF + 1024 * depth
    in_maps = []
    for (b, h) in cores:
        m = dict(shared)
        m.update(host_prep_core(depth, S, NX, x[b], c[b], h == 1, na_rel_bias))
        in_maps.append(m)
    res = run_bass_kernel_spmd(nc, in_maps, core_ids=list(range(len(cores))), **_RUN_KW)
    _LAST["res"] = res
    outs = []
    for (b, h), r in zip(cores, res.results):
        o = r["outT"].T
        outs.append(o[::-1] if h == 1 else o)
    return outs


def kernel(x, c, ln1, w_ada, b_ada, w_in, q_norm_dil, k_norm_dil, q_norm_na, k_norm_na,
           na_rel_bias, w_out, ln2, w_mlp_in, w_mlp_out):
    f = lambda a: np.asarray(a, dtype=np.float32)
    x, c = f(x), f(c)
    depth = 2
    B, S, _ = x.shape
    shared = host_prep_shared(depth, f(w_ada), f(b_ada), f(ln1), f(ln2), f(w_in), f(q_norm_dil), f(k_norm_dil),
                              f(q_norm_na), f(k_norm_na), f(w_out), f(w_mlp_in), f(w_mlp_out))
    cores = [(b, h) for b in range(B) for h in range(2)]
    outs = run_model(depth, S, S // 2, x, c, shared, f(na_rel_bias), cores)
    out = np.empty((B, S, D), np.float32)
    for (b, h), o in zip(cores, outs):
        if h == 0:
            out[b, :S // 2] = o
        else:
            out[b, S // 2:] = o
    return out
```
